# Optimizing a Trainium2 kernel written in Bass

```python
import jax, jax.numpy as jnp
from jax import lax
import numpy as np


D_MODEL = 2048
BATCH = 4
SEQ = 4096
DEPTH = 2

GRID_W = 64
CTX_LEN = 256
N_MIXERS = 2
FOURIER_GROUPS = 4
MLA_HEADS = 16
Q_LORA_RANK = 448
KV_LORA_RANK = 512
QK_NOPE_DIM = 128
QK_ROPE_DIM = 64
V_HEAD_DIM = 128
QK_HEAD_DIM = QK_NOPE_DIM + QK_ROPE_DIM
ROPE_PAIRS_PER_AXIS = QK_ROPE_DIM // 4
ROPE_THETA = 10000.0
Q_BLOCK = 128
N_EXPERTS = 32
TOP_K = 4
D_EXPERT = D_MODEL
SWIGLU_ALPHA = 1.702
SWIGLU_LIMIT = 7.0
MOE_BLOCK = 256
NORM_EPS = 1e-6

kernel_name = "hybrid_fourier_mla_moe_diffusion_block"


def _rmsnorm(x, g):
    xf = x.astype(jnp.float32)
    y = xf * lax.rsqrt(jnp.mean(xf * xf, axis=-1, keepdims=True) + NORM_EPS)
    return y.astype(x.dtype) * g


def _modulate(h, shift, scale):
    return h * (1 + scale) + shift


def _axial_rope_tables(n):
    rows = n // GRID_W
    row = jnp.repeat(jnp.arange(rows, dtype=jnp.float32), GRID_W)
    col = jnp.broadcast_to(jnp.arange(GRID_W, dtype=jnp.float32), (rows, GRID_W)).reshape(-1)
    inv = ROPE_THETA ** (-jnp.arange(ROPE_PAIRS_PER_AXIS, dtype=jnp.float32) / ROPE_PAIRS_PER_AXIS)
    ang = jnp.stack([row[:, None] * inv, col[:, None] * inv], axis=1)
    return jnp.cos(ang), jnp.sin(ang)


def _apply_rope(x, cos, sin):
    xs = x.astype(jnp.float32).reshape(x.shape[:-1] + (2, 2, ROPE_PAIRS_PER_AXIS))
    x1, x2 = xs[..., 0, :], xs[..., 1, :]
    cs = cos[None, :, None]
    sn = sin[None, :, None]
    out = jnp.stack([x1 * cs - x2 * sn, x2 * cs + x1 * sn], axis=-2)
    return out.reshape(x.shape).astype(x.dtype)


def _fourier_mix(h, w_out, b_out):
    b, n, d = h.shape
    hg = h.astype(jnp.float32).reshape(b, n, FOURIER_GROUPS, d // FOURIER_GROUPS)
    y = jnp.fft.fft2(hg, axes=(1, 3), norm="ortho").real
    return y.reshape(b, n, d).astype(h.dtype) @ w_out + b_out


def _mla_qkv(h, w_in, g_q_lora, w_q_up, g_kv_lora, w_kv_up, g_q_head, g_k_head, rope):
    b, n, _ = h.shape
    a = h @ w_in
    cq, ckv, k_pe = jnp.split(a, [Q_LORA_RANK, Q_LORA_RANK + KV_LORA_RANK], axis=-1)
    q = (_rmsnorm(cq, g_q_lora) @ w_q_up).reshape(b, n, MLA_HEADS, QK_HEAD_DIM)
    kv = (_rmsnorm(ckv, g_kv_lora) @ w_kv_up).reshape(b, n, MLA_HEADS, QK_NOPE_DIM + V_HEAD_DIM)
    k_nope, v = jnp.split(kv, [QK_NOPE_DIM], axis=-1)
    k_pe = jnp.broadcast_to(k_pe[:, :, None, :], (b, n, MLA_HEADS, QK_ROPE_DIM))
    k = jnp.concatenate([k_nope, k_pe], axis=-1)
    q = _rmsnorm(q, g_q_head)
    k = _rmsnorm(k, g_k_head)
    if rope is not None:
        cos, sin = rope
        q = jnp.concatenate([q[..., :QK_NOPE_DIM], _apply_rope(q[..., QK_NOPE_DIM:], cos, sin)], axis=-1)
        k = jnp.concatenate([k[..., :QK_NOPE_DIM], _apply_rope(k[..., QK_NOPE_DIM:], cos, sin)], axis=-1)
    return q, k, v


def _attend_dense(q, k, v):
    s = jnp.einsum("bqhd,bkhd->bhqk", q, k).astype(jnp.float32) * (QK_HEAD_DIM ** -0.5)
    p = jax.nn.softmax(s, axis=-1).astype(v.dtype)
    return jnp.einsum("bhqk,bkhd->bqhd", p, v)


def _attend_latent(q, k_lat, v_lat, k_ctx, v_ctx):
    b, n, h, dq = q.shape
    k = jnp.concatenate([k_ctx, k_lat], axis=1)
    v = jnp.concatenate([v_ctx, v_lat], axis=1)
    qb = q.reshape(b, n // Q_BLOCK, Q_BLOCK, h, dq).swapaxes(0, 1)
    o = lax.map(lambda qi: _attend_dense(qi, k, v), qb)
    return o.swapaxes(0, 1).reshape(b, n, h * V_HEAD_DIM)


def _mla_mixer(h_lat, h_ctx, cos, sin, w_in, g_q_lora, w_q_up, g_kv_lora, w_kv_up,
               g_q_head, g_k_head, w_out, with_ctx):
    q_l, k_l, v_l = _mla_qkv(h_lat, w_in, g_q_lora, w_q_up, g_kv_lora, w_kv_up,
                             g_q_head, g_k_head, (cos, sin))
    q_c, k_c, v_c = _mla_qkv(h_ctx, w_in, g_q_lora, w_q_up, g_kv_lora, w_kv_up,
                             g_q_head, g_k_head, None)
    y_lat = _attend_latent(q_l, k_l, v_l, k_c, v_c) @ w_out
    if not with_ctx:
        return y_lat, None
    b, l = h_ctx.shape[:2]
    y_ctx = _attend_dense(q_c, k_c, v_c).reshape(b, l, MLA_HEADS * V_HEAD_DIM) @ w_out
    return y_lat, y_ctx


def _clamped_swiglu(u):
    u = u.reshape(u.shape[:-1] + (D_EXPERT, 2))
    glu = jnp.minimum(u[..., 0], SWIGLU_LIMIT)
    lin = jnp.clip(u[..., 1], -SWIGLU_LIMIT, SWIGLU_LIMIT)
    return glu * jax.nn.sigmoid(SWIGLU_ALPHA * glu) * (lin + 1)


def _moe(h, w_router, b_router, w_up, b_up, w_down, b_down):
    t, d = h.shape
    logits = (h @ w_router + b_router).astype(jnp.float32)
    top_logit, top_idx = lax.top_k(logits, TOP_K)
    gates = jax.nn.softmax(top_logit, axis=-1)
    n_rows = t * TOP_K
    flat_expert = top_idx.reshape(-1)
    order = jnp.argsort(flat_expert)
    e_sorted = flat_expert[order]
    tok_sorted = order // TOP_K
    gate_sorted = gates.reshape(-1)[order]
    counts = jnp.bincount(flat_expert, length=N_EXPERTS)
    padded = (counts + MOE_BLOCK - 1) // MOE_BLOCK * MOE_BLOCK
    start = jnp.cumsum(counts) - counts
    pad_end = jnp.cumsum(padded)
    pad_start = pad_end - padded
    dest = pad_start[e_sorted] + (jnp.arange(n_rows) - start[e_sorted])
    n_blocks = -(-n_rows // MOE_BLOCK) + N_EXPERTS
    slot_tok = jnp.zeros((n_blocks * MOE_BLOCK,), jnp.int32).at[dest].set(tok_sorted.astype(jnp.int32))
    block_expert = jnp.minimum(
        jnp.searchsorted(pad_end, jnp.arange(n_blocks) * MOE_BLOCK, side="right"), N_EXPERTS - 1)

    def expert_block(args):
        tok, e = args
        u = h[tok] @ w_up[e] + b_up[e]
        return _clamped_swiglu(u) @ w_down[e] + b_down[e]

    y = lax.map(expert_block, (slot_tok.reshape(n_blocks, MOE_BLOCK), block_expert))
    y = y.reshape(-1, d)[dest] * gate_sorted[:, None].astype(h.dtype)
    return jnp.zeros_like(h).at[tok_sorted].add(y)


def setup_inputs(seed: int = 0) -> dict:
    key = jax.random.key(seed)
    ks = jax.random.split(key, 24)
    d = D_MODEL
    f32 = jnp.float32
    n_f = len(range(0, DEPTH, N_MIXERS))
    n_m = len(range(1, DEPTH, N_MIXERS))

    def nrm(k, shape, scale):
        return jax.random.normal(k, shape, f32) * scale

    def gain(k, shape):
        return 1.0 + 0.05 * jax.random.normal(k, shape, f32)

    return {
        "x": nrm(ks[0], (BATCH, SEQ, d), 1.0),
        "c": nrm(ks[1], (BATCH, d), 1.0),
        "ctx": nrm(ks[2], (BATCH, CTX_LEN, d), 1.0),
        "c_ctx": nrm(ks[3], (d,), 1.0),
        "w_mod": nrm(ks[4], (DEPTH, d, 6 * d), 0.5 * d ** -0.5),
        "b_mod": nrm(ks[5], (DEPTH, 6 * d), 0.02),
        "g_mix": gain(ks[6], (DEPTH, d)),
        "g_ffn": gain(ks[7], (DEPTH, d)),
        "fourier_w_out": nrm(ks[8], (n_f, d, d), d ** -0.5),
        "fourier_b_out": nrm(ks[9], (n_f, d), 0.02),
        "mla_w_in": nrm(ks[10], (n_m, d, Q_LORA_RANK + KV_LORA_RANK + QK_ROPE_DIM), d ** -0.5),
        "mla_g_q_lora": gain(ks[11], (n_m, Q_LORA_RANK)),
        "mla_w_q_up": nrm(ks[12], (n_m, Q_LORA_RANK, MLA_HEADS * QK_HEAD_DIM), Q_LORA_RANK ** -0.5),
        "mla_g_kv_lora": gain(ks[13], (n_m, KV_LORA_RANK)),
        "mla_w_kv_up": nrm(ks[14], (n_m, KV_LORA_RANK, MLA_HEADS * (QK_NOPE_DIM + V_HEAD_DIM)), KV_LORA_RANK ** -0.5),
        "mla_g_q_head": gain(ks[15], (n_m, QK_HEAD_DIM)),
        "mla_g_k_head": gain(ks[16], (n_m, QK_HEAD_DIM)),
        "mla_w_out": nrm(ks[17], (n_m, MLA_HEADS * V_HEAD_DIM, d), (MLA_HEADS * V_HEAD_DIM) ** -0.5),
        "router_w": nrm(ks[18], (DEPTH, d, N_EXPERTS), d ** -0.5),
        "router_b": nrm(ks[19], (DEPTH, N_EXPERTS), 0.01),
        "expert_w_up": nrm(ks[20], (DEPTH, N_EXPERTS, d, 2 * D_EXPERT), d ** -0.5),
        "expert_b_up": nrm(ks[21], (DEPTH, N_EXPERTS, 2 * D_EXPERT), 0.01),
        "expert_w_down": nrm(ks[22], (DEPTH, N_EXPERTS, D_EXPERT, d), D_EXPERT ** -0.5),
        "expert_b_down": nrm(ks[23], (DEPTH, N_EXPERTS, d), 0.01),
    }


def reference(x, c, ctx, c_ctx, w_mod, b_mod, g_mix, g_ffn, fourier_w_out, fourier_b_out,
              mla_w_in, mla_g_q_lora, mla_w_q_up, mla_g_kv_lora, mla_w_kv_up,
              mla_g_q_head, mla_g_k_head, mla_w_out, router_w, router_b,
              expert_w_up, expert_b_up, expert_w_down, expert_b_down):
    b, n, d = x.shape
    l = ctx.shape[1]
    cos, sin = _axial_rope_tables(n)
    x_lat, x_ctx = x, ctx
    for i in range(DEPTH):
        last = i == DEPTH - 1
        j = i // N_MIXERS
        mod_lat = (jax.nn.silu(c) @ w_mod[i] + b_mod[i])[:, None, :]
        mod_ctx = jax.nn.silu(c_ctx) @ w_mod[i] + b_mod[i]
        sh1, sc1, ga1, sh2, sc2, ga2 = jnp.split(mod_lat, 6, axis=-1)
        csh1, csc1, cga1, csh2, csc2, cga2 = jnp.split(mod_ctx, 6, axis=-1)

        h_lat = _modulate(_rmsnorm(x_lat, g_mix[i]), sh1, sc1)
        h_ctx = _modulate(_rmsnorm(x_ctx, g_mix[i]), csh1, csc1)
        if i % N_MIXERS == 0:
            y_lat = _fourier_mix(h_lat, fourier_w_out[j], fourier_b_out[j])
            y_ctx = None if last else _fourier_mix(h_ctx, fourier_w_out[j], fourier_b_out[j])
        else:
            y_lat, y_ctx = _mla_mixer(h_lat, h_ctx, cos, sin, mla_w_in[j], mla_g_q_lora[j],
                                      mla_w_q_up[j], mla_g_kv_lora[j], mla_w_kv_up[j],
                                      mla_g_q_head[j], mla_g_k_head[j], mla_w_out[j],
                                      not last)
        x_lat = x_lat + ga1 * y_lat

        f_lat = _modulate(_rmsnorm(x_lat, g_ffn[i]), sh2, sc2).reshape(b * n, d)
        if last:
            out = _moe(f_lat, router_w[i], router_b[i], expert_w_up[i], expert_b_up[i],
                       expert_w_down[i], expert_b_down[i])
            x_lat = x_lat + ga2 * out.reshape(b, n, d)
        else:
            x_ctx = x_ctx + cga1 * y_ctx
            f_ctx = _modulate(_rmsnorm(x_ctx, g_ffn[i]), csh2, csc2).reshape(b * l, d)
            out = _moe(jnp.concatenate([f_ctx, f_lat], axis=0), router_w[i], router_b[i],
                       expert_w_up[i], expert_b_up[i], expert_w_down[i], expert_b_down[i])
            x_ctx = x_ctx + cga2 * out[:b * l].reshape(b, l, d)
            x_lat = x_lat + ga2 * out[b * l:].reshape(b, n, d)
    return x_lat
```

```python
import numpy as np
import ml_dtypes
from contextlib import ExitStack
import concourse.bass as bass
import concourse.mybir as mybir
from concourse.bass_utils import run_bass_kernel_spmd

F32 = mybir.dt.float32
BF16 = mybir.dt.bfloat16
I32 = mybir.dt.int32
AF = mybir.ActivationFunctionType
ALU = mybir.AluOpType
AX = mybir.AxisListType

D = 2048
KC = 16
NE = 32
EPS = 1e-6
NCORES = 8


class Buf:
    __slots__ = ("w", "r")

    def __init__(self):
        self.w = None
        self.r = {}


class KB:
    def __init__(self, nc, es):
        self.nc = nc
        self.es = es
        self.eng = dict(pe=nc.tensor, act=nc.scalar, dve=nc.vector, pool=nc.gpsimd, sp=nc.sync)
        self.sem = {k: es.enter_context(nc.semaphore("s_" + k)) for k in self.eng}
        self.cnt = {k: 0 for k in self.eng}
        self.seen = {k: {} for k in self.eng}
        self.dsems = {}
        self.drr = {}
        for q, n in (("sp", 12), ("pool", 12), ("act", 4)):
            self.dsems[q] = [[es.enter_context(nc.semaphore("d_%s%d" % (q, i))), 0, "d_%s%d" % (q, i)] for i in range(n)]
            self.drr[q] = 0
        self.uid = 0

    def name(self, p):
        self.uid += 1
        return "%s_%d" % (p, self.uid)

    def sb(self, shape, dt, name="t"):
        return self.es.enter_context(self.nc.sbuf_tensor(self.name(name), list(shape), dt))

    def ps(self, shape, dt, name="p"):
        return self.es.enter_context(self.nc.psum_tensor(self.name(name), list(shape), dt))

    def dram(self, shape, dt, name="scr"):
        return self.nc.dram_tensor(self.name(name), list(shape), dt, kind="Internal").ap()

    def _wait(self, e, deps):
        for d in deps:
            if d is None:
                continue
            if d[0] == "dma":
                key, semobj, v = d[3], d[1], d[2]
            else:
                if d[0] == e and e == "pe":
                    continue
                key, semobj, v = d[0], self.sem[d[0]], d[1]
            if self.seen[e].get(key, 0) >= v:
                continue
            self.eng[e].wait_ge(semobj, v)
            self.seen[e][key] = v

    def _deps(self, rd, wr, extra):
        deps = list(extra)
        for b in rd:
            deps.append(b.w)
        for b in wr:
            deps.append(b.w)
            deps.extend(b.r.values())
        return deps

    def _commit(self, tok, rd, wr):
        key = tok[3] if tok[0] == "dma" else tok[0]
        for b in rd:
            b.r[key] = tok
        for b in wr:
            b.w = tok
            b.r = {}

    def op(self, e, fn, rd=(), wr=(), extra=()):
        self._wait(e, self._deps(rd, wr, extra))
        ins = fn()
        self.cnt[e] += 1
        ins.then_inc(self.sem[e], 1)
        tok = (e, self.cnt[e])
        self._commit(tok, rd, wr)
        return tok

    def dma(self, q, out, in_, rd=(), wr=(), extra=(), indirect=None):
        slots = self.dsems[q]
        i = self.drr[q]
        self.drr[q] = (i + 1) % len(slots)
        s = slots[i]
        deps = self._deps(rd, wr, extra)
        if s[1] > 0:
            deps.append(("dma", s[0], s[1], s[2]))
        self._wait(q, deps)
        if indirect is None:
            ins = self.eng[q].dma_start(out=out, in_=in_)
        else:
            ins = self.eng[q].indirect_dma_start(out=out, in_=in_, **indirect)
        s[1] += 16
        ins.then_inc(s[0], 16)
        tok = ("dma", s[0], s[1], s[2])
        self._commit(tok, rd, wr)
        return tok

    def barrier(self):
        toks = [(k, v) for k, v in self.cnt.items() if v > 0]
        for q in self.dsems:
            for s in self.dsems[q]:
                if s[1] > 0:
                    toks.append(("dma", s[0], s[1], s[2]))
        for e in self.eng:
            self._wait(e, [t for t in toks if not (t[0] == e)])

    class _Scope:
        def __init__(self, kb):
            self.kb = kb

        def __enter__(self):
            self.old = self.kb.es
            self.stack = ExitStack()
            self.stack.__enter__()
            self.kb.es = self.stack
            return self

        def __exit__(self, *a):
            self.kb.barrier()
            self.kb.es = self.old
            self.stack.__exit__(None, None, None)
            return False

    def scope(self):
        return KB._Scope(self)

    def finish(self, toks):
        self._wait("sp", toks)


class PsumPool:
    def __init__(self, kb, n=8):
        self.f32 = [kb.ps([128, 512], F32, "psf") for _ in range(n)]
        self.bufs = [Buf() for _ in range(n)]
        self.n = n

    def bf(self, i):
        return self.f32[i][:].bitcast(BF16)


def emit_rstd(kb, x_t, xb, junk, junkb, ss, ssb, rstd, rstdb, width):
    nc = kb.nc
    kb.op("act", lambda: nc.scalar.activation(out=junk, in_=x_t, func=AF.Square, accum_out=ss),
          rd=[xb], wr=[junkb, ssb])
    kb.op("dve", lambda: nc.vector.tensor_scalar(rstd, ss, 1.0 / width, EPS, ALU.mult, ALU.add),
          rd=[ssb], wr=[rstdb])
    kb.op("act", lambda: nc.scalar.activation(out=rstd, in_=rstd, func=AF.Sqrt),
          rd=[rstdb], wr=[rstdb])
    kb.op("dve", lambda: nc.vector.reciprocal(rstd, rstd),
          rd=[rstdb], wr=[rstdb])


def load_bcast(kb, q, dst, dstb, row_ap):
    return kb.dma(q, dst, row_ap.partition_broadcast(128), wr=[dstb])


class Consts:
    def __init__(self, kb, ident_ap):
        nc = kb.nc
        self.idf = kb.sb([128, 128], F32, "idf")
        self.idfb = Buf()
        self.idb = kb.sb([128, 128], BF16, "idb")
        self.idbb = Buf()
        self.ones_bf = kb.sb([1, 128], BF16, "ones")
        self.ones_bfb = Buf()
        kb.dma("sp", self.idf[:], ident_ap, wr=[self.idfb])
        kb.op("dve", lambda: nc.vector.tensor_copy(self.idb[:], self.idf[:]), rd=[self.idfb], wr=[self.idbb])
        kb.op("dve", lambda: nc.vector.memset(self.ones_bf[:], 1.0), wr=[self.ones_bfb])


def emit_transposes(kb, pp, cst, src, srcb, nk, banks, evac, fp32=False, lastw=128):
    nc = kb.nc
    ng = (nk + 3) // 4
    for g in range(ng):
        bank = banks[g % len(banks)]
        nj = min(4, nk - g * 4)
        if fp32:
            pv = pp.f32[bank][:]
            idt, idtb = cst.idf, cst.idfb
        else:
            pv = pp.bf(bank)
            idt, idtb = cst.idb, cst.idbb
        for j in range(nj):
            k = g * 4 + j
            w = lastw if k == nk - 1 else 128
            kb.op("pe", lambda k=k, j=j, pv=pv, idt=idt, w=w: nc.tensor.transpose(pv[0:w, j * 128:(j + 1) * 128], src[:, k * 128:k * 128 + w], idt[:]),
                  rd=[srcb, idtb], wr=[pp.bufs[bank]])
        evac(g, nj, pv[:, 0:nj * 128].rearrange("p (j t) -> p j t", j=nj), pp.bufs[bank])


class ModTiles:
    def __init__(self, kb, modr, grow_ap, nsets, sub, want_ga):
        nc = kb.nc
        o = 3 * sub
        g = kb.sb([128, D], F32, "grow"); gb = Buf()
        load_bcast(kb, "sp", g[:], gb, grow_ap)
        self.gm, self.gmb, self.sh, self.shb, self.ga, self.gab = [], [], [], [], [], []
        for s in range(nsets):
            gm = kb.sb([128, D], F32, "gm"); gmb = Buf()
            sh = kb.sb([128, D], F32, "sh"); shb = Buf()
            load_bcast(kb, "sp", gm[:], gmb, modr[s, o + 1:o + 2, :])
            load_bcast(kb, "sp", sh[:], shb, modr[s, o:o + 1, :])
            kb.op("dve", lambda gm=gm: nc.vector.scalar_tensor_tensor(gm[:], gm[:], 1.0, g[:], ALU.add, ALU.mult), rd=[gmb, gb], wr=[gmb])
            self.gm.append(gm); self.gmb.append(gmb); self.sh.append(sh); self.shb.append(shb)
            if want_ga:
                ga = kb.sb([128, D], F32, "ga"); gab = Buf()
                load_bcast(kb, "sp", ga[:], gab, modr[s, o + 2:o + 3, :])
                self.ga.append(ga); self.gab.append(gab)


class NormMod:
    def __init__(self, kb):
        self.kb = kb
        self.junk = kb.sb([128, D], BF16, "junk"); self.junkb = Buf()
        self.ss = kb.sb([128, 1], F32, "ss"); self.ssb = Buf()
        self.rs = kb.sb([128, 1], F32, "rs"); self.rsb = Buf()
        self.tmp = kb.sb([128, D], F32, "nmtmp"); self.tmpb = Buf()

    def emit(self, x, xb, mt, s, out, outb):
        kb, nc = self.kb, self.kb.nc
        emit_rstd(kb, x, xb, self.junk[:], self.junkb, self.ss[:], self.ssb, self.rs[:], self.rsb, D)
        kb.op("dve", lambda: nc.vector.scalar_tensor_tensor(self.tmp[:], x, self.rs[:], mt.gm[s][:], ALU.mult, ALU.mult),
              rd=[xb, self.rsb, mt.gmb[s]], wr=[self.tmpb])
        kb.op("pool", lambda: nc.gpsimd.tensor_tensor(out, self.tmp[:], mt.sh[s][:], ALU.add), rd=[self.tmpb, mt.shb[s]], wr=[outb])


def emit_mods(kb, pp, ccT, wm, bm, mo, NCOL):
    nc = kb.nc
    c32 = kb.sb([128, KC, 5], F32, "c32"); c32b = Buf()
    sT = kb.sb([128, KC, 5], BF16, "sT"); sTb = Buf()
    kb.dma("sp", c32[:], ccT, wr=[c32b])
    kb.op("act", lambda: nc.scalar.activation(out=sT[:], in_=c32[:], func=AF.Silu), rd=[c32b], wr=[sTb])
    Wt = [kb.sb([128, KC, 512], BF16, "Wm") for _ in range(2)]
    Wb = [Buf(), Buf()]
    bt = kb.sb([5, 2, NCOL], F32, "bmt"); btb = Buf()
    ot = kb.sb([5, 2, NCOL], F32, "mot"); otb = Buf()
    for l in range(2):
        kb.dma("sp", bt[:, l, :], bm[l:l + 1, :].partition_broadcast(5), wr=[btb])
    i = 0
    for l in range(2):
        for nb in range(NCOL // 512):
            w = i % 2
            bank = i % 2
            i += 1
            kb.dma("pool", Wt[w][:], wm[l].rearrange("(k p) n -> p k n", p=128)[:, :, nb * 512:(nb + 1) * 512], wr=[Wb[w]])
            ps = pp.f32[bank]
            for k in range(KC):
                kb.op("pe", lambda k=k, w=w, ps=ps: nc.tensor.matmul(ps[0:5, :], sT[:, k, :], Wt[w][:, k, :], start=(k == 0), stop=(k == KC - 1)),
                      rd=[sTb, Wb[w]], wr=[pp.bufs[bank]])
            kb.op("dve", lambda l=l, nb=nb, ps=ps: nc.vector.tensor_tensor(ot[:, l, nb * 512:(nb + 1) * 512], ps[0:5, :], bt[:, l, nb * 512:(nb + 1) * 512], ALU.add),
                  rd=[pp.bufs[bank], btb], wr=[otb])
    return [kb.dma("sp", mo.rearrange("l r n -> r l n"), ot[:], rd=[otb])]


def emit_moe_pre(kb, pp, cst, T, sets, nsets, xsrc, xsrcb, modr, gffn, rw, rb, f_out, gates_out):
    nc = kb.nc
    outs = []
    with kb.scope():
        mt = ModTiles(kb, modr, gffn, nsets, 1, False)
        nm = NormMod(kb)
        xt = kb.sb([128, D], F32, "xt"); xtb = Buf()
        f32t = kb.sb([128, D], F32, "f32t"); f32b = Buf()
        fbf = kb.sb([128, D], BF16, "fbf"); fbfb = Buf()
        fT32 = kb.sb([128, KC, 128], F32, "fT32"); fT32b = Buf()
        rwt = kb.sb([128, KC, NE], F32, "rwt"); rwb = Buf()
        rbt = kb.sb([128, NE], F32, "rbt"); rbb = Buf()
        lgt = kb.sb([128, NE], F32, "lgt"); lgb = Buf()
        exq = kb.sb([128, NE], F32, "exq"); exb = Buf()
        gtmp = kb.sb([128, NE], F32, "gtmp"); gtb = Buf()
        gout = kb.sb([128, NE], F32, "gout"); goutb = Buf()
        v8 = kb.sb([128, 8], F32, "v8"); v8b = Buf()
        nv0 = kb.sb([128, 1], F32, "nv0"); nv0b = Buf()
        ssum = kb.sb([128, 1], F32, "ssum"); ssumb = Buf()
        kb.dma("sp", rwt[:], rw.rearrange("(k p) n -> p k n", p=128), wr=[rwb])
        load_bcast(kb, "sp", rbt[:], rbb, rb)
        for t in range(T):
            s = sets[t]
            kb.dma("sp", xt[:], xsrc[t * 128:(t + 1) * 128, :], rd=[xsrcb], wr=[xtb])
            nm.emit(xt[:], xtb, mt, s, f32t[:], f32b)
            kb.op("pool", lambda: nc.gpsimd.tensor_copy(fbf[:], f32t[:]), rd=[f32b], wr=[fbfb])
            outs.append(kb.dma("sp", f_out[t * 128:(t + 1) * 128, :], fbf[:], rd=[fbfb]))

            def ev(g, nj, pv, bb):
                kb.op("dve", lambda: nc.vector.tensor_copy(fT32[:, g * 4:g * 4 + nj, :], pv), rd=[bb], wr=[fT32b])
            emit_transposes(kb, pp, cst, f32t, f32b, KC, [0, 1], ev, fp32=True)
            ps = pp.f32[2]
            for k in range(KC):
                kb.op("pe", lambda k=k: nc.tensor.matmul(ps[:, 0:NE], fT32[:, k, :], rwt[:, k, :], start=(k == 0), stop=(k == KC - 1)),
                      rd=[fT32b, rwb], wr=[pp.bufs[2]])
            kb.op("dve", lambda: nc.vector.tensor_tensor(lgt[:], ps[:, 0:NE], rbt[:], ALU.add), rd=[pp.bufs[2], rbb], wr=[lgb])
            kb.op("dve", lambda: nc.vector.max(out=v8[:], in_=lgt[:]), rd=[lgb], wr=[v8b])
            kb.op("dve", lambda: nc.vector.tensor_scalar(nv0[:], v8[:, 0:1], -1.0, None, ALU.mult), rd=[v8b], wr=[nv0b])
            kb.op("act", lambda: nc.scalar.activation(out=exq[:], in_=lgt[:], func=AF.Exp, bias=nv0[:], scale=1.0), rd=[lgb, nv0b], wr=[exb])
            kb.op("dve", lambda: nc.vector.scalar_tensor_tensor(gtmp[:], lgt[:], v8[:, 3:4], exq[:], ALU.is_ge, ALU.mult, accum_out=ssum[:]),
                  rd=[lgb, v8b, exb], wr=[gtb, ssumb])
            kb.op("dve", lambda: nc.vector.reciprocal(ssum[:], ssum[:]), rd=[ssumb], wr=[ssumb])
            kb.op("dve", lambda: nc.vector.tensor_scalar(gout[:], gtmp[:], ssum[:], None, ALU.mult), rd=[gtb, ssumb], wr=[goutb])
            outs.append(kb.dma("sp", gates_out[t * 128:(t + 1) * 128, :], gout[:], rd=[goutb]))
    return outs


def emit_experts(kb, pp, cst, NT, NEL, fT, gts, wup, bup, wdn, bdn, yp, G=6):
    nc = kb.nc
    outs = []
    fTg = kb.sb([128, KC, G * 128], BF16, "fTg"); fTgb = Buf()
    acc = kb.sb([128, G, D], F32, "acc"); accb = [Buf() for _ in range(G)]
    act = kb.sb([128, G, D], BF16, "act"); actb = [Buf() for _ in range(G)]
    actT = kb.sb([128, KC, G * 128], BF16, "actT"); actTb = Buf()
    NW = 2
    Wt = [kb.sb([128, KC + 1, 512], BF16, "Wt") for _ in range(NW)]
    Wb = [Buf() for _ in range(NW)]
    Bb = [Buf() for _ in range(NW)]
    gt = kb.sb([128, G, NEL], F32, "gt"); gtb = Buf()
    g1 = kb.sb([128, 256], F32, "g1"); g1b = Buf()
    sg = kb.sb([128, 256], F32, "sg"); sgb = Buf()
    l1 = kb.sb([128, 256], F32, "l1"); l1b = Buf()
    ob = kb.sb([128, D], BF16, "ob"); obb = Buf()
    wi = 0
    upi = 0
    dni = 0
    UPB = [2, 3, 4]
    DNB = [5, 6, 7]
    t0 = 0
    while t0 < NT:
        ng = min(G, NT - t0)
        kb.dma("sp", fTg[:, :, 0:ng * 128], fT[:, :, t0 * 128:(t0 + ng) * 128], wr=[fTgb])
        kb.dma("sp", gt[:, 0:ng, :], gts[t0 * 128:(t0 + ng) * 128, :].rearrange("(g p) e -> p g e", p=128), wr=[gtb])
        for gi in range(ng):
            kb.op("pool", lambda gi=gi: nc.gpsimd.memset(acc[:, gi, :], 0.0), wr=[accb[gi]])
        for e in range(NEL):
            for n in range(8):
                w = wi % NW
                wi += 1
                kb.dma("pool", Wt[w][:, 0:KC, :], wup[e].rearrange("(k p) n -> p k n", p=128)[:, :, n * 512:(n + 1) * 512], wr=[Wb[w]])
                kb.dma("pool", Wt[w][0:1, KC, :], bup[e:e + 1, n * 512:(n + 1) * 512], wr=[Bb[w]])
                for gi in range(ng):
                    bank = UPB[upi % 3]
                    upi += 1
                    ps = pp.f32[bank]
                    kb.op("pe", lambda ps=ps, w=w: nc.tensor.matmul(ps[:], cst.ones_bf[0:1, :], Wt[w][0:1, KC, :], start=True, stop=False),
                          rd=[cst.ones_bfb, Bb[w]], wr=[pp.bufs[bank]])
                    for k in range(KC):
                        kb.op("pe", lambda ps=ps, w=w, k=k, gi=gi: nc.tensor.matmul(ps[:], fTg[:, k, gi * 128:(gi + 1) * 128], Wt[w][:, k, :],
                                                                                start=False, stop=(k == KC - 1)),
                              rd=[fTgb, Wb[w]], wr=[pp.bufs[bank]])
                    pv = ps[:].rearrange("p (j two) -> p j two", two=2)
                    kb.op("dve", lambda pv=pv: nc.vector.tensor_scalar(g1[:], pv[:, :, 0], 7.0, None, ALU.min), rd=[pp.bufs[bank]], wr=[g1b])
                    kb.op("dve", lambda pv=pv: nc.vector.tensor_scalar(l1[:], pv[:, :, 1], 1.0, 8.0, ALU.add, ALU.min), rd=[pp.bufs[bank]], wr=[l1b])
                    kb.op("act", lambda: nc.scalar.activation(out=sg[:], in_=g1[:], func=AF.Sigmoid, scale=1.702), rd=[g1b], wr=[sgb])
                    kb.op("dve", lambda: nc.vector.scalar_tensor_tensor(l1[:], l1[:], -6.0, g1[:], ALU.max, ALU.mult), rd=[l1b, g1b], wr=[l1b])
                    kb.op("dve", lambda gi=gi, n=n: nc.vector.tensor_tensor(act[:, gi, n * 256:(n + 1) * 256], sg[:], l1[:], ALU.mult),
                          rd=[sgb, l1b], wr=[actb[gi]])
            for gi in range(ng):
                def ev2(g, nj, pv, bb, gi=gi):
                    kb.op("act", lambda: nc.scalar.copy(actT[:, g * 4:g * 4 + nj, gi * 128:(gi + 1) * 128], pv), rd=[bb], wr=[actTb])
                emit_transposes(kb, pp, cst, act[:, gi, :], actb[gi], KC, [0, 1], ev2)
            for n in range(4):
                w = wi % NW
                wi += 1
                kb.dma("pool", Wt[w][:, 0:KC, :], wdn[e].rearrange("(k p) n -> p k n", p=128)[:, :, n * 512:(n + 1) * 512], wr=[Wb[w]])
                kb.dma("pool", Wt[w][0:1, KC, :], bdn[e:e + 1, n * 512:(n + 1) * 512], wr=[Bb[w]])
                for gi in range(ng):
                    bank = DNB[dni % 3]
                    dni += 1
                    ps = pp.f32[bank]
                    kb.op("pe", lambda ps=ps, w=w: nc.tensor.matmul(ps[:], cst.ones_bf[0:1, :], Wt[w][0:1, KC, :], start=True, stop=False),
                          rd=[cst.ones_bfb, Bb[w]], wr=[pp.bufs[bank]])
                    for k in range(KC):
                        kb.op("pe", lambda ps=ps, w=w, k=k, gi=gi: nc.tensor.matmul(ps[:], actT[:, k, gi * 128:(gi + 1) * 128], Wt[w][:, k, :],
                                                                                start=False, stop=(k == KC - 1)),
                              rd=[actTb, Wb[w]], wr=[pp.bufs[bank]])
                    kb.op("dve", lambda ps=ps, gi=gi, n=n, e=e: nc.vector.scalar_tensor_tensor(
                        acc[:, gi, n * 512:(n + 1) * 512], ps[:], gt[:, gi, e:e + 1], acc[:, gi, n * 512:(n + 1) * 512], ALU.mult, ALU.add),
                        rd=[pp.bufs[bank], gtb, accb[gi]], wr=[accb[gi]])
        for gi in range(ng):
            kb.op("pool", lambda gi=gi: nc.gpsimd.tensor_copy(ob[:], acc[:, gi, :]), rd=[accb[gi]], wr=[obb])
            outs.append(kb.dma("sp", yp[(t0 + gi) * 128:(t0 + gi + 1) * 128, :], ob[:], rd=[obb]))
        t0 += ng
    return outs


def emit_combine(kb, T, sets, nsets, parts, xsrc, modr, xout):
    nc = kb.nc
    outs = []
    ga, gab = [], []
    for s in range(nsets):
        g = kb.sb([128, D], F32, "ga2"); gb = Buf()
        load_bcast(kb, "sp", g[:], gb, modr[s, 5:6, :])
        ga.append(g); gab.append(gb)
    pt = kb.sb([128, NCORES, D], BF16, "pt"); ptb = Buf()
    xt = kb.sb([128, D], F32, "xt"); xtb = Buf()
    a = kb.sb([128, D], F32, "a"); ab = Buf()
    for t in range(T):
        s = sets[t]
        kb.dma("sp", pt[:], parts[:, t * 128:(t + 1) * 128, :].rearrange("c p d -> p c d"), wr=[ptb])
        kb.dma("sp", xt[:], xsrc[t * 128:(t + 1) * 128, :], wr=[xtb])
        kb.op("dve", lambda: nc.vector.tensor_tensor(a[:], pt[:, 0, :], pt[:, 1, :], ALU.add), rd=[ptb], wr=[ab])
        for c in range(2, NCORES):
            kb.op("dve", lambda c=c: nc.vector.tensor_tensor(a[:], a[:], pt[:, c, :], ALU.add), rd=[ptb, ab], wr=[ab])
        kb.op("dve", lambda s=s: nc.vector.tensor_tensor(a[:], a[:], ga[s][:], ALU.mult), rd=[ab, gab[s]], wr=[ab])
        kb.op("dve", lambda: nc.vector.tensor_tensor(a[:], a[:], xt[:], ALU.add), rd=[ab, xtb], wr=[ab])
        outs.append(kb.dma("sp", xout[t * 128:(t + 1) * 128, :], a[:], rd=[ab]))
    return outs


def emit_fourier(kb, pp, cst, xb, ctxb, xown, modr, gmix, cs512, cn, sn, cnc, snc, wo, bo, x1out):
    nc = kb.nc
    NTL, NTC = 32, 2
    Zd = [kb.dram([(NTL + NTC) * 128, D], BF16, "Zc"), kb.dram([(NTL + NTC) * 128, D], BF16, "Zs")]
    outs = []
    with kb.scope():
        mt = ModTiles(kb, modr, gmix, 2, 0, False)
        nm = NormMod(kb)
        xt = kb.sb([128, D], F32, "xt"); xtb = Buf()
        hb = kb.sb([128, D], BF16, "hb"); hbb = Buf()
        hT = kb.sb([128, KC, 128], BF16, "hT"); hTb = Buf()
        C = kb.sb([128, 2, 4, 512], BF16, "C512"); Cb = Buf()
        kb.dma("sp", C[:], cs512.rearrange("c (kk p) j -> p c kk j", p=128), wr=[Cb])
        zt = [kb.sb([128, D], BF16, "zt") for _ in range(2)]
        ztb = [Buf(), Buf()]
        bi = 0
        for t in range(NTL + NTC):
            s = 0 if t < NTL else 1
            src = xb[t * 128:(t + 1) * 128, :] if t < NTL else ctxb[(t - NTL) * 128:(t - NTL + 1) * 128, :]
            kb.dma("sp", xt[:], src, wr=[xtb])
            nm.emit(xt[:], xtb, mt, s, hb[:], hbb)

            def ev(g, nj, pv, bb):
                kb.op("act", lambda: nc.scalar.copy(hT[:, g * 4:g * 4 + nj, :], pv), rd=[bb], wr=[hTb])
            emit_transposes(kb, pp, cst, hb, hbb, KC, [0, 1], ev)
            for c in range(2):
                for g in range(4):
                    bank = 2 + bi % 4
                    bi += 1
                    ps = pp.f32[bank]
                    for kk in range(4):
                        kb.op("pe", lambda ps=ps, g=g, kk=kk, c=c: nc.tensor.matmul(ps[:], hT[:, g * 4 + kk, :], C[:, c, kk, :], start=(kk == 0), stop=(kk == 3)),
                              rd=[hTb, Cb], wr=[pp.bufs[bank]])
                    eng = "dve" if (g % 2 == 0) else "act"
                    if eng == "dve":
                        kb.op("dve", lambda ps=ps, g=g, c=c: nc.vector.tensor_copy(zt[c][:, g * 512:(g + 1) * 512], ps[:]), rd=[pp.bufs[bank]], wr=[ztb[c]])
                    else:
                        kb.op("act", lambda ps=ps, g=g, c=c: nc.scalar.copy(zt[c][:, g * 512:(g + 1) * 512], ps[:]), rd=[pp.bufs[bank]], wr=[ztb[c]])
                kb.dma("sp", Zd[c][t * 128:(t + 1) * 128, :], zt[c][:], rd=[ztb[c]])
    with kb.scope():
        PB = 256
        CNb = kb.sb([128, NTL, PB], BF16, "CNb"); CNbb = Buf()
        SNb = kb.sb([128, NTL, PB], BF16, "SNb"); SNbb = Buf()
        zc = [kb.sb([128, NTL, 128], BF16, "zc") for _ in range(2)]; zcb = [Buf(), Buf()]
        zs = [kb.sb([128, NTL, 128], BF16, "zs") for _ in range(2)]; zsb = [Buf(), Buf()]
        FT = kb.sb([128, KC, PB], BF16, "FT"); FTb = Buf()
        Wo = [kb.sb([128, KC, 512], BF16, "Wo") for _ in range(2)]; Wob = [Buf(), Buf()]
        xo = kb.sb([128, 2, D], F32, "xo"); xob = Buf()
        oo = kb.sb([128, 2, D], F32, "oo"); oob = Buf()
        ga = []; gab = []
        for s in range(2):
            g = kb.sb([128, D], F32, "ga1"); gb = Buf()
            load_bcast(kb, "sp", g[:], gb, modr[s, 2:3, :])
            ga.append(g); gab.append(gb)
        bot = kb.sb([128, D], F32, "bot"); botb = Buf()
        load_bcast(kb, "sp", bot[:], botb, bo)
        cnt = {"z": 0, "w": 0, "b2": 0, "b3": 0}

        def fpos(nt0, ntn, cnsrc, snsrc, npos, row0, s):
            hh = max(1, ntn // 2)
            for a0 in range(0, ntn, hh):
                kb.dma("sp", CNb[:, a0:a0 + hh, 0:npos], cnsrc.rearrange("(nt p) k -> p nt k", p=128)[:, a0:a0 + hh, :], wr=[CNbb])
                kb.dma("sp", SNb[:, a0:a0 + hh, 0:npos], snsrc.rearrange("(nt p) k -> p nt k", p=128)[:, a0:a0 + hh, :], wr=[SNbb])
            ntile = npos // 128
            for pt in range(ntile):
                kb.dma("sp", xo[:, pt, :], xown[row0 + pt * 128:row0 + (pt + 1) * 128, :], wr=[xob])
            for cb in range(KC):
                i = cnt["z"] % 2
                cnt["z"] += 1
                for (zt_, ztb_, Zsrc) in ((zc[i], zcb[i], Zd[0]), (zs[i], zsb[i], Zd[1])):
                    zv = Zsrc.rearrange("(nt p) c -> p nt c", p=128)
                    h = max(1, ntn // 2)
                    for a0 in range(0, ntn, h):
                        kb.dma("sp", zt_[:, a0:a0 + h, :], zv[:, nt0 + a0:nt0 + a0 + h, cb * 128:(cb + 1) * 128], wr=[ztb_])
                bank = 2 + cnt["b2"] % 2
                cnt["b2"] += 1
                ps = pp.f32[bank]
                n_mm = 2 * ntn
                j = 0
                for (zt_, ztb_, Mb, Mbb) in ((zc[i], zcb[i], CNb, CNbb), (zs[i], zsb[i], SNb, SNbb)):
                    for nt in range(ntn):
                        kb.op("pe", lambda ps=ps, zt_=zt_, Mb=Mb, nt=nt, j=j: nc.tensor.matmul(ps[:, 0:npos], zt_[:, nt, :], Mb[:, nt, 0:npos],
                                                                                         start=(j == 0), stop=(j == n_mm - 1)),
                              rd=[ztb_, Mbb], wr=[pp.bufs[bank]])
                        j += 1
                kb.op("act", lambda ps=ps, cb=cb: nc.scalar.copy(FT[:, cb, 0:npos], ps[:, 0:npos]), rd=[pp.bufs[bank]], wr=[FTb])
            for nb in range(4):
                w = cnt["w"] % 2
                cnt["w"] += 1
                kb.dma("pool", Wo[w][:], wo.rearrange("(k p) n -> p k n", p=128)[:, :, nb * 512:(nb + 1) * 512], wr=[Wob[w]])
                for pt in range(ntile):
                    bank = 4 + cnt["b3"] % 2
                    cnt["b3"] += 1
                    ps = pp.f32[bank]
                    for cc in range(KC):
                        kb.op("pe", lambda ps=ps, cc=cc, pt=pt, w=w: nc.tensor.matmul(ps[:], FT[:, cc, pt * 128:(pt + 1) * 128], Wo[w][:, cc, :],
                                                                                 start=(cc == 0), stop=(cc == KC - 1)),
                              rd=[FTb, Wob[w]], wr=[pp.bufs[bank]])
                    sl = slice(nb * 512, (nb + 1) * 512)
                    kb.op("dve", lambda ps=ps, pt=pt, sl=sl: nc.vector.tensor_tensor(oo[:, pt, sl], ps[:], bot[:, sl], ALU.add), rd=[pp.bufs[bank], botb], wr=[oob])
                    kb.op("dve", lambda pt=pt, sl=sl: nc.vector.tensor_tensor(oo[:, pt, sl], oo[:, pt, sl], ga[s][:, sl], ALU.mult), rd=[oob, gab[s]], wr=[oob])
                    kb.op("dve", lambda pt=pt, sl=sl: nc.vector.tensor_tensor(oo[:, pt, sl], oo[:, pt, sl], xo[:, pt, sl], ALU.add), rd=[oob, xob], wr=[oob])
            for pt in range(ntile):
                outs.append(kb.dma("sp", x1out[row0 + pt * 128:row0 + (pt + 1) * 128, :], oo[:, pt, :], rd=[oob]))

        for pb in range(2048 // PB):
            fpos(0, NTL, cn[:, pb * PB:(pb + 1) * PB], sn[:, pb * PB:(pb + 1) * PB], PB, pb * PB, 0)
        fpos(NTL, NTC, cnc, snc, 128, 2048, 1)
    return outs


def _dft(n, scale):
    i = np.arange(n, dtype=np.int64)
    m = (i[:, None] * i[None, :]) % n
    ang = 2.0 * np.pi * m.astype(np.float64) / n
    return (np.cos(ang) * scale), (np.sin(ang) * scale)


def host_consts():
    c512, s512 = _dft(512, 512 ** -0.5)
    cN, sN = _dft(4096, 4096 ** -0.5)
    cC, sC = _dft(256, 256 ** -0.5)
    bf = ml_dtypes.bfloat16
    return dict(cs512=np.stack([c512, s512]).astype(np.float32).astype(bf),
                cn=cN.astype(np.float32).astype(bf), sn=(-sN).astype(np.float32).astype(bf),
                cnc=cC.astype(np.float32).astype(bf), snc=(-sC).astype(np.float32).astype(bf))


def _rope(kb, dst1, dst2, dstb, x1, x2, xb_, cos, sin, csb, t1, t2, tb):
    nc = kb.nc
    kb.op("dve", lambda: nc.vector.tensor_tensor(t1, x1, cos, ALU.mult), rd=[xb_, csb], wr=[tb])
    kb.op("dve", lambda: nc.vector.tensor_tensor(t2, x2, sin, ALU.mult), rd=[xb_, csb], wr=[tb])
    kb.op("dve", lambda: nc.vector.tensor_tensor(dst1, t1, t2, ALU.subtract), rd=[tb], wr=[dstb])
    kb.op("dve", lambda: nc.vector.tensor_tensor(t1, x2, cos, ALU.mult), rd=[xb_, csb], wr=[tb])
    kb.op("dve", lambda: nc.vector.tensor_tensor(t2, x1, sin, ALU.mult), rd=[xb_, csb], wr=[tb])
    kb.op("dve", lambda: nc.vector.tensor_tensor(dst2, t1, t2, ALU.add), rd=[tb], wr=[dstb])


def _head_rstd(kb, ss, ssb, rstd, rstdb, extra=None, extrab=None):
    nc = kb.nc
    if extra is not None:
        kb.op("dve", lambda: nc.vector.tensor_scalar(rstd, ss, extra, None, ALU.add), rd=[ssb, extrab], wr=[rstdb])
        kb.op("dve", lambda: nc.vector.tensor_scalar(rstd, rstd, 1.0 / 192, EPS, ALU.mult, ALU.add), rd=[rstdb], wr=[rstdb])
    else:
        kb.op("dve", lambda: nc.vector.tensor_scalar(rstd, ss, 1.0 / 192, EPS, ALU.mult, ALU.add), rd=[ssb], wr=[rstdb])
    kb.op("act", lambda: nc.scalar.activation(out=rstd, in_=rstd, func=AF.Sqrt), rd=[rstdb], wr=[rstdb])
    kb.op("dve", lambda: nc.vector.reciprocal(rstd, rstd), rd=[rstdb], wr=[rstdb])


def emit_mla(kb, pp, cst, xb2, ctxb2, xown, modr, gmix, w_in, gql, wqu, gkl, wkvu, gqh, gkh, wout, cosk, sink, cosq, sinq, x3out):
    nc = kb.nc
    NTC, NTL, NQ, NH = 2, 32, 16, 16
    NK = NTC + NTL
    Kd = kb.dram([NK * 128, NH, 192], BF16, "Kd")
    Vd = kb.dram([NK * 128, NH, 128], BF16, "Vd")
    Qd = kb.dram([NQ * 128, NH, 192], BF16, "Qd")
    Ad = kb.dram([NQ * 128, D], BF16, "Ad")
    outs = []
    with kb.scope():
        mt = ModTiles(kb, modr, gmix, 2, 0, False)
        nm = NormMod(kb)
        xt = kb.sb([128, D], F32, "xt"); xtb = Buf()
        hb = kb.sb([128, D], BF16, "hb"); hbb = Buf()
        hT = kb.sb([128, KC, 128], BF16, "hT"); hTb = Buf()
        win = kb.sb([128, KC, 576], BF16, "win"); winb = Buf()
        kb.dma("pool", win[:], w_in.rearrange("(k p) n -> p k n", p=128)[:, :, 448:1024], wr=[winb])
        wkv = kb.sb([128, 4, 4096], BF16, "wkv"); wkvb = Buf()
        for kk in range(4):
            kb.dma("pool", wkv[:, kk, :], wkvu[kk * 128:(kk + 1) * 128, :], wr=[wkvb])
        gklt = kb.sb([128, 512], F32, "gklt"); gklb = Buf()
        load_bcast(kb, "sp", gklt[:], gklb, gkl)
        gkht = kb.sb([128, 192], F32, "gkht"); gkhb = Buf()
        load_bcast(kb, "sp", gkht[:], gkhb, gkh)
        a = kb.sb([128, 576], F32, "a"); ab = Buf()
        ckvn = kb.sb([128, 512], BF16, "ckvn"); ckvnb = Buf()
        ckvT = kb.sb([128, 4, 128], BF16, "ckvT"); ckvTb = Buf()
        kv = kb.sb([128, NH, 256], F32, "kv"); kvb = Buf()
        sq = kb.sb([128, NH, 128], F32, "sq"); sqb = Buf()
        junk = kb.sb([128, 512], BF16, "junk5"); junkb = Buf()
        ss = kb.sb([128, 1], F32, "ss5"); ssb = Buf()
        rs = kb.sb([128, 1], F32, "rs5"); rsb = Buf()
        sspe = kb.sb([128, 1], F32, "sspe"); sspeb = Buf()
        ssk = kb.sb([128, NH], F32, "ssk"); sskb = Buf()
        rk = kb.sb([128, NH], F32, "rk"); rkb = Buf()
        kg = kb.sb([128, 64], F32, "kg"); kgb = Buf()
        kr = kb.sb([128, 64], F32, "kr"); krb = Buf()
        t1 = kb.sb([128, 32], F32, "t1"); t2 = kb.sb([128, 32], F32, "t2"); tb = Buf()
        cs = kb.sb([128, 2, 32], F32, "cs"); csb = Buf()
        Kt = kb.sb([128, NH, 192], BF16, "Kt"); Ktb = Buf()
        Vt = kb.sb([128, NH, 128], BF16, "Vt"); Vtb = Buf()
        bi = 0
        for t in range(NK):
            isctx = t < NTC
            s = 1 if isctx else 0
            src = ctxb2[t * 128:(t + 1) * 128, :] if isctx else xb2[(t - NTC) * 128:(t - NTC + 1) * 128, :]
            kb.dma("sp", xt[:], src, wr=[xtb])
            nm.emit(xt[:], xtb, mt, s, hb[:], hbb)

            def ev(g, nj, pv, bb):
                kb.op("act", lambda: nc.scalar.copy(hT[:, g * 4:g * 4 + nj, :], pv), rd=[bb], wr=[hTb])
            emit_transposes(kb, pp, cst, hb, hbb, KC, [0, 1], ev)
            for (c0, c1, bank) in ((0, 512, 2), (512, 576, 3)):
                ps = pp.f32[bank]
                for k in range(KC):
                    kb.op("pe", lambda ps=ps, k=k, c0=c0, c1=c1: nc.tensor.matmul(ps[:, 0:c1 - c0], hT[:, k, :], win[:, k, c0:c1], start=(k == 0), stop=(k == KC - 1)),
                          rd=[hTb, winb], wr=[pp.bufs[bank]])
                kb.op("dve", lambda ps=ps, c0=c0, c1=c1: nc.vector.tensor_copy(a[:, c0:c1], ps[:, 0:c1 - c0]), rd=[pp.bufs[bank]], wr=[ab])
            emit_rstd(kb, a[:, 0:512], ab, junk[:], junkb, ss[:], ssb, rs[:], rsb, 512)
            kb.op("dve", lambda: nc.vector.scalar_tensor_tensor(ckvn[:], a[:, 0:512], rs[:], gklt[:], ALU.mult, ALU.mult), rd=[ab, rsb, gklb], wr=[ckvnb])

            def ev3(g, nj, pv, bb):
                kb.op("act", lambda: nc.scalar.copy(ckvT[:, g * 4:g * 4 + nj, :], pv), rd=[bb], wr=[ckvTb])
            emit_transposes(kb, pp, cst, ckvn, ckvnb, 4, [0, 1], ev3)
            kvf = kv[:].rearrange("p h d -> p (h d)")
            for nb in range(8):
                bank = 4 + bi % 4
                bi += 1
                ps = pp.f32[bank]
                for kk in range(4):
                    kb.op("pe", lambda ps=ps, kk=kk, nb=nb: nc.tensor.matmul(ps[:], ckvT[:, kk, :], wkv[:, kk, nb * 512:(nb + 1) * 512], start=(kk == 0), stop=(kk == 3)),
                          rd=[ckvTb, wkvb], wr=[pp.bufs[bank]])
                if nb % 2 == 0:
                    kb.op("dve", lambda ps=ps, nb=nb: nc.vector.tensor_copy(kvf[:, nb * 512:(nb + 1) * 512], ps[:]), rd=[pp.bufs[bank]], wr=[kvb])
                else:
                    kb.op("act", lambda ps=ps, nb=nb: nc.scalar.copy(kvf[:, nb * 512:(nb + 1) * 512], ps[:]), rd=[pp.bufs[bank]], wr=[kvb])
            kn = kv[:, :, 0:128]
            kb.op("pool", lambda: nc.gpsimd.tensor_tensor(sq[:], kn, kn, ALU.mult), rd=[kvb], wr=[sqb])
            kb.op("dve", lambda: nc.vector.tensor_reduce(out=ssk[:], in_=sq[:], axis=AX.X, op=ALU.add), rd=[sqb], wr=[sskb])
            kb.op("act", lambda: nc.scalar.activation(out=junk[:, 0:64], in_=a[:, 512:576], func=AF.Square, accum_out=sspe[:]), rd=[ab], wr=[junkb, sspeb])
            _head_rstd(kb, ssk[:], sskb, rk[:], rkb, sspe[:], sspeb)
            kb.op("dve", lambda: nc.vector.tensor_tensor(sq[:], kn, rk[:].unsqueeze(2).to_broadcast([128, NH, 128]), ALU.mult), rd=[kvb, rkb, sqb], wr=[sqb])
            kb.op("pool", lambda: nc.gpsimd.tensor_tensor(Kt[:, :, 0:128], sq[:], gkht[:, 0:128].unsqueeze(1).to_broadcast([128, NH, 128]), ALU.mult),
                  rd=[sqb, gkhb], wr=[Ktb])
            kb.op("dve", lambda: nc.vector.tensor_tensor(kg[:], a[:, 512:576], gkht[:, 128:192], ALU.mult), rd=[ab, gkhb], wr=[kgb])
            if isctx:
                krv = kg
                krvb = kgb
            else:
                li = t - NTC
                kb.dma("sp", cs[:, 0, :], cosk[li * 128:(li + 1) * 128, :], wr=[csb])
                kb.dma("sp", cs[:, 1, :], sink[li * 128:(li + 1) * 128, :], wr=[csb])
                kg4 = kg[:].rearrange("p (a h i) -> p a h i", a=2, h=2)
                kr4 = kr[:].rearrange("p (a h i) -> p a h i", a=2, h=2)
                cv = cs[:, 0, :].rearrange("p (a i) -> p a i", a=2)
                sv = cs[:, 1, :].rearrange("p (a i) -> p a i", a=2)
                tv1 = t1[:].rearrange("p (a i) -> p a i", a=2)
                tv2 = t2[:].rearrange("p (a i) -> p a i", a=2)
                _rope(kb, kr4[:, :, 0, :], kr4[:, :, 1, :], krb, kg4[:, :, 0, :], kg4[:, :, 1, :], kgb, cv, sv, csb, tv1, tv2, tb)
                krv = kr
                krvb = krb
            kb.op("dve", lambda krv=krv: nc.vector.tensor_tensor(Kt[:, :, 128:192], krv[:].unsqueeze(1).to_broadcast([128, NH, 64]),
                                                                rk[:].unsqueeze(2).to_broadcast([128, NH, 64]), ALU.mult),
                  rd=[krvb, rkb], wr=[Ktb])
            kb.op("pool", lambda: nc.gpsimd.tensor_copy(Vt[:], kv[:, :, 128:256]), rd=[kvb], wr=[Vtb])
            kb.dma("sp", Kd[t * 128:(t + 1) * 128, :, :], Kt[:], rd=[Ktb])
            kb.dma("sp", Vd[t * 128:(t + 1) * 128, :, :], Vt[:], rd=[Vtb])
    with kb.scope():
        mt = ModTiles(kb, modr, gmix, 1, 0, False)
        nm = NormMod(kb)
        xt = kb.sb([128, D], F32, "xt"); xtb = Buf()
        hb = kb.sb([128, D], BF16, "hb"); hbb = Buf()
        hT = kb.sb([128, KC, 128], BF16, "hT"); hTb = Buf()
        win = kb.sb([128, KC, 448], BF16, "winq"); winb = Buf()
        kb.dma("pool", win[:], w_in.rearrange("(k p) n -> p k n", p=128)[:, :, 0:448], wr=[winb])
        wq = kb.sb([128, 4, 3072], BF16, "wq"); wqb = Buf()
        for kk in range(4):
            K_ = 128 if kk < 3 else 64
            kb.dma("pool", wq[0:K_, kk, :], wqu[kk * 128:kk * 128 + K_, :], wr=[wqb])
        gqlt = kb.sb([128, 448], F32, "gqlt"); gqlb = Buf()
        load_bcast(kb, "sp", gqlt[:], gqlb, gql)
        gqht = kb.sb([128, 192], F32, "gqht"); gqhb = Buf()
        load_bcast(kb, "sp", gqht[:], gqhb, gqh)
        kb.op("dve", lambda: nc.vector.tensor_scalar(gqht[:], gqht[:], 192 ** -0.5, None, ALU.mult), rd=[gqhb], wr=[gqhb])
        a = kb.sb([128, 448], F32, "aq"); ab = Buf()
        cqn = kb.sb([128, 512], BF16, "cqn"); cqnb = Buf()
        kb.op("dve", lambda: nc.vector.memset(cqn[:], 0.0), wr=[cqnb])
        cqT = kb.sb([128, 4, 128], BF16, "cqT"); cqTb = Buf()
        q = kb.sb([128, NH, 192], F32, "q"); qb_ = Buf()
        sq = kb.sb([128, NH, 192], F32, "sqq"); sqb = Buf()
        junk = kb.sb([128, 448], BF16, "junkq"); junkb = Buf()
        ss = kb.sb([128, 1], F32, "ssq1"); ssb = Buf()
        rs = kb.sb([128, 1], F32, "rsq1"); rsb = Buf()
        ssh = kb.sb([128, NH], F32, "ssh"); sshb = Buf()
        rq = kb.sb([128, NH], F32, "rq"); rqb = Buf()
        t1 = kb.sb([128, NH, 16], F32, "t1q"); t2 = kb.sb([128, NH, 16], F32, "t2q"); tb = Buf()
        cs = kb.sb([128, 2, 32], F32, "csq"); csb = Buf()
        Qt = kb.sb([128, NH, 192], BF16, "Qt"); Qtb = Buf()
        bi = 0
        for t in range(NQ):
            kb.dma("sp", xt[:], xown[t * 128:(t + 1) * 128, :], wr=[xtb])
            nm.emit(xt[:], xtb, mt, 0, hb[:], hbb)

            def ev(g, nj, pv, bb):
                kb.op("act", lambda: nc.scalar.copy(hT[:, g * 4:g * 4 + nj, :], pv), rd=[bb], wr=[hTb])
            emit_transposes(kb, pp, cst, hb, hbb, KC, [0, 1], ev)
            ps = pp.f32[2]
            for k in range(KC):
                kb.op("pe", lambda k=k: nc.tensor.matmul(ps[:, 0:448], hT[:, k, :], win[:, k, :], start=(k == 0), stop=(k == KC - 1)),
                      rd=[hTb, winb], wr=[pp.bufs[2]])
            kb.op("dve", lambda: nc.vector.tensor_copy(a[:], ps[:, 0:448]), rd=[pp.bufs[2]], wr=[ab])
            emit_rstd(kb, a[:], ab, junk[:], junkb, ss[:], ssb, rs[:], rsb, 448)
            kb.op("dve", lambda: nc.vector.scalar_tensor_tensor(cqn[:, 0:448], a[:], rs[:], gqlt[:], ALU.mult, ALU.mult), rd=[ab, rsb, gqlb], wr=[cqnb])

            def ev3(g, nj, pv, bb):
                kb.op("act", lambda: nc.scalar.copy(cqT[:, g * 4:g * 4 + nj, :], pv), rd=[bb], wr=[cqTb])
            emit_transposes(kb, pp, cst, cqn, cqnb, 4, [0, 1], ev3)
            qf = q[:].rearrange("p h d -> p (h d)")
            for nb in range(6):
                bank = 4 + bi % 4
                bi += 1
                ps2 = pp.f32[bank]
                for kk in range(4):
                    K_ = 128 if kk < 3 else 64
                    kb.op("pe", lambda ps2=ps2, kk=kk, nb=nb, K_=K_: nc.tensor.matmul(ps2[:], cqT[0:K_, kk, :], wq[0:K_, kk, nb * 512:(nb + 1) * 512],
                                                                                   start=(kk == 0), stop=(kk == 3)),
                          rd=[cqTb, wqb], wr=[pp.bufs[bank]])
                if nb % 2 == 0:
                    kb.op("dve", lambda ps2=ps2, nb=nb: nc.vector.tensor_copy(qf[:, nb * 512:(nb + 1) * 512], ps2[:]), rd=[pp.bufs[bank]], wr=[qb_])
                else:
                    kb.op("act", lambda ps2=ps2, nb=nb: nc.scalar.copy(qf[:, nb * 512:(nb + 1) * 512], ps2[:]), rd=[pp.bufs[bank]], wr=[qb_])
            kb.op("pool", lambda: nc.gpsimd.tensor_tensor(sq[:], q[:], q[:], ALU.mult), rd=[qb_], wr=[sqb])
            kb.op("dve", lambda: nc.vector.tensor_reduce(out=ssh[:], in_=sq[:], axis=AX.X, op=ALU.add), rd=[sqb], wr=[sshb])
            _head_rstd(kb, ssh[:], sshb, rq[:], rqb)
            kb.op("dve", lambda: nc.vector.tensor_tensor(sq[:], q[:], rq[:].unsqueeze(2).to_broadcast([128, NH, 192]), ALU.mult), rd=[qb_, rqb, sqb], wr=[sqb])
            kb.op("pool", lambda: nc.gpsimd.tensor_tensor(q[:], sq[:], gqht[:].unsqueeze(1).to_broadcast([128, NH, 192]), ALU.mult), rd=[sqb, gqhb], wr=[qb_])
            kb.dma("sp", cs[:, 0, :], cosq[t * 128:(t + 1) * 128, :], wr=[csb])
            kb.dma("sp", cs[:, 1, :], sinq[t * 128:(t + 1) * 128, :], wr=[csb])
            for ax in range(2):
                o = 128 + ax * 32
                cv = cs[:, 0, ax * 16:(ax + 1) * 16].unsqueeze(1).to_broadcast([128, NH, 16])
                sv = cs[:, 1, ax * 16:(ax + 1) * 16].unsqueeze(1).to_broadcast([128, NH, 16])
                _rope(kb, Qt[:, :, o:o + 16], Qt[:, :, o + 16:o + 32], Qtb, q[:, :, o:o + 16], q[:, :, o + 16:o + 32], qb_, cv, sv, csb, t1[:], t2[:], tb)
            kb.op("pool", lambda: nc.gpsimd.tensor_copy(Qt[:, :, 0:128], q[:, :, 0:128]), rd=[qb_], wr=[Qtb])
            kb.dma("sp", Qd[t * 128:(t + 1) * 128, :, :], Qt[:], rd=[Qtb])
    with kb.scope():
        Kh = kb.sb([128, NK, 192], BF16, "Kh"); Khb = Buf()
        Qh = kb.sb([128, NQ, 192], BF16, "Qh"); Qhb = Buf()
        Vh = kb.sb([128, NK, 132], BF16, "Vh"); Vhb = Buf()
        kb.op("dve", lambda: nc.vector.memset(Vh[:], 1.0), wr=[Vhb])
        KT = kb.sb([128, NK * 128], BF16, "KT"); KTb = Buf()
        KTp = kb.sb([64, NK * 128], BF16, "KTp"); KTpb = Buf()
        QT = kb.sb([128, NQ * 128], BF16, "QT"); QTb = Buf()
        QTp = kb.sb([64, NQ * 128], BF16, "QTp"); QTpb = Buf()
        PT = [kb.sb([128, 512], BF16, "PT") for _ in range(3)]; PTb = [Buf() for _ in range(3)]
        rec = kb.sb([128, 1], F32, "rec"); recb = Buf()
        at = [kb.sb([128, 128], BF16, "at") for _ in range(2)]; atb = [Buf(), Buf()]
        Kv = Kd.rearrange("(nt p) h d -> p nt h d", p=128)
        Vv = Vd.rearrange("(nt p) h d -> p nt h d", p=128)
        Qv = Qd.rearrange("(nt p) h d -> p nt h d", p=128)
        pti = 0
        sbi = 0
        ati = 0
        for hd in range(NH):
            for a0 in range(0, NK, 17):
                kb.dma("sp", Kh[:, a0:a0 + 17, :], Kv[:, a0:a0 + 17, hd, :], wr=[Khb])
                kb.dma("sp", Vh[:, a0:a0 + 17, 0:128], Vv[:, a0:a0 + 17, hd, :], wr=[Vhb])
            kb.dma("sp", Qh[:], Qv[:, :, hd, :], wr=[Qhb])
            for (src, srcb, n, dT, dTb, dP, dPb) in ((Kh, Khb, NK, KT, KTb, KTp, KTpb), (Qh, Qhb, NQ, QT, QTb, QTp, QTpb)):
                for p0 in range(0, n, 2):
                    bank = (p0 // 2) % 2
                    pv = pp.bf(bank)
                    for j in range(2):
                        nt = p0 + j
                        kb.op("pe", lambda pv=pv, j=j, nt=nt, src=src: nc.tensor.transpose(pv[:, (2 * j) * 128:(2 * j + 1) * 128], src[:, nt, 0:128], cst.idb[:]),
                              rd=[srcb, cst.idbb], wr=[pp.bufs[bank]])
                        kb.op("pe", lambda pv=pv, j=j, nt=nt, src=src: nc.tensor.transpose(pv[0:64, (2 * j + 1) * 128:(2 * j + 2) * 128], src[:, nt, 128:192], cst.idb[:]),
                              rd=[srcb, cst.idbb], wr=[pp.bufs[bank]])
                    pv4 = pv[:, 0:512].rearrange("p (t two c) -> p t two c", two=2, c=128)
                    if (p0 // 2) % 2 == 0:
                        kb.op("act", lambda pv4=pv4, p0=p0, dT=dT: nc.scalar.copy(dT[:, p0 * 128:(p0 + 2) * 128].rearrange("p (t c) -> p t c", c=128), pv4[:, :, 0, :]),
                              rd=[pp.bufs[bank]], wr=[dTb])
                        kb.op("act", lambda pv4=pv4, p0=p0, dP=dP: nc.scalar.copy(dP[0:64, p0 * 128:(p0 + 2) * 128].rearrange("p (t c) -> p t c", c=128), pv4[0:64, :, 1, :]),
                              rd=[pp.bufs[bank]], wr=[dPb])
                    else:
                        kb.op("dve", lambda pv4=pv4, p0=p0, dT=dT: nc.vector.tensor_copy(dT[:, p0 * 128:(p0 + 2) * 128].rearrange("p (t c) -> p t c", c=128), pv4[:, :, 0, :]),
                              rd=[pp.bufs[bank]], wr=[dTb])
                        kb.op("dve", lambda pv4=pv4, p0=p0, dP=dP: nc.vector.tensor_copy(dP[0:64, p0 * 128:(p0 + 2) * 128].rearrange("p (t c) -> p t c", c=128), pv4[0:64, :, 1, :]),
                              rd=[pp.bufs[bank]], wr=[dPb])
            for qb in range(NQ // 4):
                for kt in range(NK):
                    sbank = 2 + sbi % 2
                    sbi += 1
                    ps = pp.f32[sbank]
                    kb.op("pe", lambda ps=ps, kt=kt, qb=qb: nc.tensor.matmul(ps[:], KT[:, kt * 128:(kt + 1) * 128], QT[:, qb * 512:(qb + 1) * 512], start=True, stop=False),
                          rd=[KTb, QTb], wr=[pp.bufs[sbank]])
                    kb.op("pe", lambda ps=ps, kt=kt, qb=qb: nc.tensor.matmul(ps[:], KTp[0:64, kt * 128:(kt + 1) * 128], QTp[0:64, qb * 512:(qb + 1) * 512], start=False, stop=True),
                          rd=[KTpb, QTpb], wr=[pp.bufs[sbank]])
                    pi = pti % 3
                    pti += 1
                    kb.op("act", lambda ps=ps, pi=pi: nc.scalar.activation(out=PT[pi][:], in_=ps[:], func=AF.Exp), rd=[pp.bufs[sbank]], wr=[PTb[pi]])
                    for qi in range(4):
                        ob = 4 + qi
                        kb.op("pe", lambda qi=qi, pi=pi, kt=kt, ob=ob: nc.tensor.matmul(pp.f32[ob][:, 0:129], PT[pi][:, qi * 128:(qi + 1) * 128], Vh[:, kt, 0:129],
                                                                                    start=(kt == 0), stop=(kt == NK - 1)),
                              rd=[PTb[pi], Vhb], wr=[pp.bufs[ob]])
                for qi in range(4):
                    ob = 4 + qi
                    ai = ati % 2
                    ati += 1
                    kb.op("dve", lambda ob=ob: nc.vector.reciprocal(rec[:], pp.f32[ob][:, 128:129]), rd=[pp.bufs[ob]], wr=[recb])
                    kb.op("dve", lambda ob=ob, ai=ai: nc.vector.tensor_scalar(at[ai][:], pp.f32[ob][:, 0:128], rec[:], None, ALU.mult), rd=[pp.bufs[ob], recb], wr=[atb[ai]])
                    r0 = (qb * 4 + qi) * 128
                    kb.dma("sp", Ad[r0:r0 + 128, hd * 128:(hd + 1) * 128], at[ai][:], rd=[atb[ai]])
    with kb.scope():
        wo = kb.sb([128, KC, D], BF16, "wo"); wob = Buf()
        for nb in range(4):
            kb.dma("pool", wo[:, :, nb * 512:(nb + 1) * 512], wout.rearrange("(k p) n -> p k n", p=128)[:, :, nb * 512:(nb + 1) * 512], wr=[wob])
        ga = kb.sb([128, D], F32, "ga1"); gab = Buf()
        load_bcast(kb, "sp", ga[:], gab, modr[0, 2:3, :])
        att = kb.sb([128, D], BF16, "att"); attb = Buf()
        aT = kb.sb([128, KC, 128], BF16, "aT"); aTb = Buf()
        xt = kb.sb([128, D], F32, "xt"); xtb = Buf()
        oo = kb.sb([128, D], F32, "oo"); oob = Buf()
        bi = 0
        for t in range(NQ):
            kb.dma("sp", att[:], Ad[t * 128:(t + 1) * 128, :], wr=[attb])
            kb.dma("sp", xt[:], xown[t * 128:(t + 1) * 128, :], wr=[xtb])

            def ev(g, nj, pv, bb):
                kb.op("act", lambda: nc.scalar.copy(aT[:, g * 4:g * 4 + nj, :], pv), rd=[bb], wr=[aTb])
            emit_transposes(kb, pp, cst, att, attb, KC, [0, 1], ev)
            for nb in range(4):
                bank = 2 + bi % 4
                bi += 1
                ps = pp.f32[bank]
                for cc in range(KC):
                    kb.op("pe", lambda ps=ps, cc=cc, nb=nb: nc.tensor.matmul(ps[:], aT[:, cc, :], wo[:, cc, nb * 512:(nb + 1) * 512], start=(cc == 0), stop=(cc == KC - 1)),
                          rd=[aTb, wob], wr=[pp.bufs[bank]])
                sl = slice(nb * 512, (nb + 1) * 512)
                kb.op("dve", lambda ps=ps, sl=sl: nc.vector.tensor_tensor(oo[:, sl], ps[:], ga[:, sl], ALU.mult), rd=[pp.bufs[bank], gab], wr=[oob])
                kb.op("dve", lambda sl=sl: nc.vector.tensor_tensor(oo[:, sl], oo[:, sl], xt[:, sl], ALU.add), rd=[oob, xtb], wr=[oob])
            outs.append(kb.dma("sp", x3out[t * 128:(t + 1) * 128, :], oo[:], rd=[oob]))
    return outs


def rope_tables(n=4096, grid_w=64, pairs=16, theta=10000.0):
    rows = n // grid_w
    row = np.repeat(np.arange(rows, dtype=np.float32), grid_w)
    col = np.broadcast_to(np.arange(grid_w, dtype=np.float32), (rows, grid_w)).reshape(-1)
    inv = (theta ** (-np.arange(pairs, dtype=np.float32) / pairs)).astype(np.float32)
    ang = np.stack([row[:, None] * inv, col[:, None] * inv], axis=1).astype(np.float32)
    return np.cos(ang).reshape(n, 32).astype(np.float32), np.sin(ang).reshape(n, 32).astype(np.float32)


def _prog(body, ins, outs):
    nc = bass.Bass("TRN2", target_bir_lowering=False)
    aps = {n: nc.dram_tensor(n, list(s), dt, kind="ExternalInput").ap() for n, (s, dt) in ins.items()}
    oaps = {n: nc.dram_tensor(n, list(s), dt, kind="ExternalOutput").ap() for n, (s, dt) in outs.items()}
    with ExitStack() as es:
        kb = KB(nc, es)
        pp = PsumPool(kb)
        cst = Consts(kb, aps["ident"])
        toks = body(kb, pp, cst, aps, oaps)
        kb.finish(toks)
    return nc


def _run(nc, in_maps):
    res = run_bass_kernel_spmd(nc, in_maps, core_ids=list(range(NCORES)))
    return res.results


NCOL = 6 * D // NCORES


def prog_mods():
    def body(kb, pp, cst, a, o):
        return emit_mods(kb, pp, a["ccT"], a["wm"], a["bm"], o["mo"], NCOL)
    return _prog(body, dict(ccT=([128, KC, 5], F32), wm=([2, D, NCOL], F32), bm=([2, NCOL], F32), ident=([128, 128], F32)),
                 dict(mo=([2, 5, NCOL], F32)))


def prog_fourier():
    T = 17

    def body(kb, pp, cst, a, o):
        toks = emit_fourier(kb, pp, cst, a["xb"], a["ctxb"], a["xown"], a["modr"], a["gmix"], a["cs512"], a["cn"], a["sn"], a["cnc"], a["snc"],
                            a["wo"], a["bo"], o["x1"])
        toks += emit_moe_pre(kb, pp, cst, T, [0] * 16 + [1], 2, o["x1"], Buf(), a["modr"], a["gffn"], a["rw"], a["rb"], o["fo"], o["go"])
        return toks
    return _prog(body, dict(xb=([4096, D], F32), ctxb=([256, D], F32), xown=([T * 128, D], F32), modr=([2, 6, D], F32), gmix=([1, D], F32),
                            gffn=([1, D], F32), cs512=([2, 512, 512], BF16), cn=([4096, 2048], BF16), sn=([4096, 2048], BF16),
                            cnc=([256, 128], BF16), snc=([256, 128], BF16), wo=([D, D], F32), bo=([1, D], F32), rw=([D, NE], F32),
                            rb=([1, NE], F32), ident=([128, 128], F32)),
                 dict(x1=([T * 128, D], F32), fo=([T * 128, D], BF16), go=([T * 128, NE], F32)))


def prog_experts(NT):
    NEL = NE // NCORES

    def body(kb, pp, cst, a, o):
        return emit_experts(kb, pp, cst, NT, NEL, a["fT"], a["gts"], a["wup"], a["bup"], a["wdn"], a["bdn"], o["yp"], G=6)
    return _prog(body, dict(fT=([128, KC, NT * 128], BF16), gts=([NT * 128, NEL], F32), wup=([NEL, D, 2 * D], F32), bup=([NEL, 2 * D], F32),
                            wdn=([NEL, D, D], F32), bdn=([NEL, D], F32), ident=([128, 128], F32)),
                 dict(yp=([NT * 128, D], BF16)))


def prog_combine(T, sets, nsets):
    def body(kb, pp, cst, a, o):
        return emit_combine(kb, T, sets, nsets, a["parts"], a["xsrc"], a["modr"], o["xo"])
    return _prog(body, dict(parts=([NCORES, T * 128, D], BF16), xsrc=([T * 128, D], F32), modr=([2, 6, D], F32), ident=([128, 128], F32)),
                 dict(xo=([T * 128, D], F32)))


def prog_mla():
    T = 16

    def body(kb, pp, cst, a, o):
        toks = emit_mla(kb, pp, cst, a["xb2"], a["ctxb2"], a["xown"], a["modr"], a["gmix"], a["w_in"], a["gql"], a["wqu"], a["gkl"], a["wkvu"],
                        a["gqh"], a["gkh"], a["wout"], a["cosk"], a["sink"], a["cosq"], a["sinq"], o["x3"])
        toks += emit_moe_pre(kb, pp, cst, T, [0] * 16, 1, o["x3"], Buf(), a["modr"], a["gffn"], a["rw"], a["rb"], o["fo"], o["go"])
        return toks
    return _prog(body, dict(xb2=([4096, D], F32), ctxb2=([256, D], F32), xown=([T * 128, D], F32), modr=([2, 6, D], F32), gmix=([1, D], F32),
                            gffn=([1, D], F32), w_in=([D, 1024], F32), gql=([1, 448], F32), wqu=([448, 3072], F32), gkl=([1, 512], F32),
                            wkvu=([512, 4096], F32), gqh=([1, 192], F32), gkh=([1, 192], F32), wout=([D, D], F32),
                            cosk=([4096, 32], F32), sink=([4096, 32], F32), cosq=([2048, 32], F32), sinq=([2048, 32], F32),
                            rw=([D, NE], F32), rb=([1, NE], F32), ident=([128, 128], F32)),
                 dict(x3=([T * 128, D], F32), fo=([T * 128, D], BF16), go=([T * 128, NE], F32)))


def _ca(x):
    return np.ascontiguousarray(x)


def _moe_layer(li, T, x_cores, fo, go, modrs, sets, nsets, inp, ident):
    NEL = NE // NCORES
    f_all = np.concatenate(fo, 0)
    NT = f_all.shape[0] // 128
    fT = _ca(f_all.reshape(NT * 128, KC, 128).transpose(2, 1, 0))
    g_all = np.concatenate(go, 0)
    ins = []
    for c in range(NCORES):
        e0 = c * NEL
        ins.append(dict(fT=fT, gts=_ca(g_all[:, e0:e0 + NEL]), wup=_ca(inp["expert_w_up"][li, e0:e0 + NEL]), bup=_ca(inp["expert_b_up"][li, e0:e0 + NEL]),
                        wdn=_ca(inp["expert_w_down"][li, e0:e0 + NEL]), bdn=_ca(inp["expert_b_down"][li, e0:e0 + NEL]), ident=ident))
    r = _run(prog_experts(NT), ins)
    yps = [rr["yp"] for rr in r]
    ins = []
    for c in range(NCORES):
        parts = _ca(np.stack([yp[c * T * 128:(c + 1) * T * 128] for yp in yps], 0))
        ins.append(dict(parts=parts, xsrc=x_cores[c], modr=modrs[c], ident=ident))
    r = _run(prog_combine(T, sets, nsets), ins)
    return [rr["xo"] for rr in r]


def kernel(**inp):
    inp = {k: np.asarray(v) for k, v in inp.items()}
    x, c, ctx, c_ctx = inp["x"], inp["c"], inp["ctx"], inp["c_ctx"]
    ident = np.eye(128, dtype=np.float32)
    hc = host_consts()
    cos, sin = rope_tables()
    cc = np.concatenate([c, c_ctx[None, :]], 0)
    ccT = _ca(cc.reshape(5, KC, 128).transpose(2, 1, 0))
    ins = [dict(ccT=ccT, wm=_ca(inp["w_mod"][:, :, k * NCOL:(k + 1) * NCOL]), bm=_ca(inp["b_mod"][:, k * NCOL:(k + 1) * NCOL]), ident=ident) for k in range(NCORES)]
    r = _run(prog_mods(), ins)
    mo = np.concatenate([rr["mo"] for rr in r], 2)

    def modr(l, b):
        return _ca(np.stack([mo[l, b].reshape(6, D), mo[l, 4].reshape(6, D)], 0))
    ins = []
    for k in range(NCORES):
        b, h = divmod(k, 2)
        ins.append(dict(xb=_ca(x[b]), ctxb=_ca(ctx[b]), xown=_ca(np.concatenate([x[b, h * 2048:(h + 1) * 2048], ctx[b, h * 128:(h + 1) * 128]], 0)),
                        modr=modr(0, b), gmix=_ca(inp["g_mix"][0:1]), gffn=_ca(inp["g_ffn"][0:1]), cs512=hc["cs512"],
                        cn=_ca(hc["cn"][:, h * 2048:(h + 1) * 2048]), sn=_ca(hc["sn"][:, h * 2048:(h + 1) * 2048]),
                        cnc=_ca(hc["cnc"][:, h * 128:(h + 1) * 128]), snc=_ca(hc["snc"][:, h * 128:(h + 1) * 128]),
                        wo=_ca(inp["fourier_w_out"][0]), bo=_ca(inp["fourier_b_out"][0:1]), rw=_ca(inp["router_w"][0]), rb=_ca(inp["router_b"][0:1]), ident=ident))
    r = _run(prog_fourier(), ins)
    x1 = [rr["x1"] for rr in r]
    x2 = _moe_layer(0, 17, x1, [rr["fo"] for rr in r], [rr["go"] for rr in r], [modr(0, k // 2) for k in range(NCORES)], [0] * 16 + [1], 2, inp, ident)
    ins = []
    for k in range(NCORES):
        b, h = divmod(k, 2)
        xb2 = np.concatenate([x2[2 * b][:2048], x2[2 * b + 1][:2048]], 0)
        cb2 = np.concatenate([x2[2 * b][2048:], x2[2 * b + 1][2048:]], 0)
        ins.append(dict(xb2=_ca(xb2), ctxb2=_ca(cb2), xown=_ca(x2[k][:2048]), modr=modr(1, b), gmix=_ca(inp["g_mix"][1:2]), gffn=_ca(inp["g_ffn"][1:2]),
                        w_in=_ca(inp["mla_w_in"][0]), gql=_ca(inp["mla_g_q_lora"][0:1]), wqu=_ca(inp["mla_w_q_up"][0]), gkl=_ca(inp["mla_g_kv_lora"][0:1]),
                        wkvu=_ca(inp["mla_w_kv_up"][0]), gqh=_ca(inp["mla_g_q_head"][0:1]), gkh=_ca(inp["mla_g_k_head"][0:1]), wout=_ca(inp["mla_w_out"][0]),
                        cosk=cos, sink=sin, cosq=_ca(cos[h * 2048:(h + 1) * 2048]), sinq=_ca(sin[h * 2048:(h + 1) * 2048]),
                        rw=_ca(inp["router_w"][1]), rb=_ca(inp["router_b"][1:2]), ident=ident))
    r = _run(prog_mla(), ins)
    x3 = [rr["x3"] for rr in r]
    x4 = _moe_layer(1, 16, x3, [rr["fo"] for rr in r], [rr["go"] for rr in r], [modr(1, k // 2) for k in range(NCORES)], [0] * 16, 1, inp, ident)
    out = np.zeros((4, 4096, D), np.float32)
    for k in range(NCORES):
        b, h = divmod(k, 2)
        out[b, h * 2048:(h + 1) * 2048] = x4[k]
    return out
```
